# Optimizing a Trainium2 kernel written in Bass

```python
import math
import jax, jax.numpy as jnp
from jax import lax
import numpy as np

D_MODEL = 1024
BATCH = 32
SEQ = 2048
DEPTH = 2

N_META = 16
EPS = 1e-6
A_HEADS = 4
A_DK = 128
A_DV = 128
A_CONV = 4
A_CHUNK = 64
A_WK = A_HEADS * A_DK
A_WV = A_HEADS * A_DV
B_HEADS = 4
B_DK = 128
B_DV = 128
B_CHUNK = 16
B_WK = B_HEADS * B_DK
B_WV = B_HEADS * B_DV
PROJ_SPLITS = (A_WK, A_WK, A_WV, A_HEADS, A_HEADS, A_WV, B_WK, B_WK, B_WV, B_WV, D_MODEL, D_MODEL)
PROJ_WIDTH = sum(PROJ_SPLITS)

kernel_name = 'hybrid_gdn_hgrn2_gated_merge'


def _rmsnorm(x, w):
    xf = x.astype(jnp.float32)
    y = xf * lax.rsqrt(jnp.mean(xf * xf, axis=-1, keepdims=True) + EPS)
    return (y * w.astype(jnp.float32)).astype(x.dtype)


def _l2norm(x):
    xf = x.astype(jnp.float32)
    return xf * lax.rsqrt(jnp.sum(xf * xf, axis=-1, keepdims=True) + EPS)


def _causal_conv(x, w):
    k, c = w.shape
    return lax.conv_general_dilated(
        x, w[:, None, :].astype(x.dtype), window_strides=(1,), padding=[(k - 1, 0)],
        dimension_numbers=('NWC', 'WIO', 'NWC'), feature_group_count=c)


def _masked_exp(diff, mask):
    return jnp.where(mask, jnp.exp(jnp.where(mask, diff, 0.0)), 0.0)


def _chunked_scan(step, state0, inputs, chunk):
    meta = tuple(a[:, :N_META] for a in inputs)
    real = tuple(a[:, N_META:] for a in inputs)
    state, out_meta = step(state0, meta)
    b, s = real[0].shape[:2]
    n = s // chunk
    xs = tuple(jnp.moveaxis(a.reshape(b, n, chunk, *a.shape[2:]), 1, 0) for a in real)
    _, out_real = lax.scan(step, state, xs)
    out_real = jnp.moveaxis(out_real, 0, 1).reshape(b, s, *out_real.shape[3:])
    return jnp.concatenate([out_meta, out_real], axis=1)


def _gdn_step(S, inp):
    q, k, v, beta, g = inp
    q, k, v = (jnp.swapaxes(a, 1, 2) for a in (q, k, v))
    beta, g = jnp.swapaxes(beta, 1, 2), jnp.swapaxes(g, 1, 2)
    L = q.shape[2]
    causal = jnp.tril(jnp.ones((L, L), dtype=bool))
    strict = jnp.tril(jnp.ones((L, L), dtype=bool), k=-1)
    G = jnp.cumsum(g, axis=-1)
    decay = _masked_exp(G[..., :, None] - G[..., None, :], causal)
    kb = k * beta[..., None]
    l_mat = jnp.where(strict, jnp.einsum('bhik,bhjk->bhij', kb, k) * decay, 0.0)
    t_mat = l_mat + jnp.eye(L, dtype=l_mat.dtype)
    rhs = jnp.concatenate([v * beta[..., None], kb * jnp.exp(G)[..., None]], axis=-1)
    sol = lax.linalg.triangular_solve(t_mat, rhs, left_side=True, lower=True, unit_diagonal=True)
    dv = v.shape[-1]
    u, w = sol[..., :dv], sol[..., dv:]
    v_new = u - jnp.einsum('bhik,bhkv->bhiv', w, S)
    scores = jnp.einsum('bhik,bhjk->bhij', q, k) * decay
    o = (jnp.einsum('bhik,bhkv->bhiv', q * jnp.exp(G)[..., None], S)
         + jnp.einsum('bhij,bhjv->bhiv', scores, v_new))
    g_last = G[..., -1:]
    S = (S * jnp.exp(g_last)[..., None]
         + jnp.einsum('bhjk,bhjv->bhkv', k * jnp.exp(g_last - G)[..., None], v_new))
    return S, jnp.swapaxes(o, 1, 2)


def _hgrn2_step(S, inp):
    q, k, v, log_f = (jnp.swapaxes(a, 1, 2) for a in inp)
    L = q.shape[2]
    causal = jnp.tril(jnp.ones((L, L), dtype=bool))
    Bc = jnp.cumsum(log_f, axis=2)
    pair = _masked_exp(Bc[:, :, :, None, :] - Bc[:, :, None, :, :],
                       causal[:, :, None])
    scores = jnp.einsum('bhik,bhjk,bhijk->bhij', q, k, pair)
    o = (jnp.einsum('bhik,bhkv->bhiv', q * jnp.exp(Bc), S)
         + jnp.einsum('bhij,bhjv->bhiv', scores, v))
    b_last = Bc[:, :, -1:]
    S = (S * jnp.exp(b_last[:, :, 0])[..., None]
         + jnp.einsum('bhjk,bhjv->bhkv', k * jnp.exp(b_last - Bc), v))
    return S, jnp.swapaxes(o, 1, 2)


def _layer(h, norm_w, w_in, conv_w, a_log, dt_bias, gnorm_a, gnorm_b, lb,
           w_branch_a, w_branch_b, w_out):
    b, t, _ = h.shape
    f32 = jnp.float32
    xn = _rmsnorm(h, norm_w)
    proj = xn @ w_in.astype(h.dtype)
    offsets = np.cumsum(PROJ_SPLITS)[:-1].tolist()
    (a_q, a_k, a_v, a_beta, a_alpha, a_z,
     b_q, b_f, b_i, b_g, gate_a, gate_b) = jnp.split(proj, offsets, axis=-1)

    qkv = jax.nn.silu(_causal_conv(jnp.concatenate([a_q, a_k, a_v], axis=-1), conv_w))
    q, k, v = jnp.split(qkv, [A_WK, 2 * A_WK], axis=-1)
    q = _l2norm(q.reshape(b, t, A_HEADS, A_DK)) * (A_DK ** -0.5)
    k = _l2norm(k.reshape(b, t, A_HEADS, A_DK))
    v = v.reshape(b, t, A_HEADS, A_DV).astype(f32)
    beta = jax.nn.sigmoid(a_beta.astype(f32))
    g = -jnp.exp(a_log.astype(f32)) * jax.nn.softplus(a_alpha.astype(f32) + dt_bias.astype(f32))
    s0_a = jnp.zeros((b, A_HEADS, A_DK, A_DV), f32)
    o_a = _chunked_scan(_gdn_step, s0_a, (q, k, v, beta, g), A_CHUNK)
    y_a = _rmsnorm(o_a, gnorm_a) * jax.nn.silu(a_z.astype(f32).reshape(b, t, A_HEADS, A_DV))
    y_a = y_a.reshape(b, t, A_WV).astype(h.dtype)

    qb = (jax.nn.silu(b_q.astype(f32)) * (B_DK ** -0.5)).reshape(b, t, B_HEADS, B_DK)
    lbf = lb.astype(f32)
    pos = lbf > 0.0
    log_sig = jax.nn.log_sigmoid(b_f.astype(f32))
    log_f = jnp.where(pos,
                      jnp.logaddexp(jnp.log(jnp.where(pos, lbf, 1.0)), jnp.log1p(-lbf) + log_sig),
                      log_sig)
    kb_ = -jnp.expm1(log_f)
    log_f = log_f.reshape(b, t, B_HEADS, B_DK)
    kb_ = kb_.reshape(b, t, B_HEADS, B_DK)
    vb = b_i.astype(f32).reshape(b, t, B_HEADS, B_DV)
    s0_b = jnp.zeros((b, B_HEADS, B_DK, B_DV), f32)
    o_b = _chunked_scan(_hgrn2_step, s0_b, (qb, kb_, vb, log_f), B_CHUNK)
    y_b = _rmsnorm(o_b, gnorm_b) * jax.nn.silu(b_g.astype(f32).reshape(b, t, B_HEADS, B_DV))
    y_b = y_b.reshape(b, t, B_WV).astype(h.dtype)

    y_a = y_a @ w_branch_a.astype(h.dtype)
    y_b = y_b @ w_branch_b.astype(h.dtype)
    mixed = jax.nn.sigmoid(gate_a) * y_a + jax.nn.sigmoid(gate_b) * y_b
    return h + mixed @ w_out.astype(h.dtype)


def setup_inputs(seed: int = 0) -> dict:
    key = jax.random.key(seed)
    ks = jax.random.split(key, 15)
    nrm = jax.random.normal
    x = nrm(ks[0], (BATCH, SEQ, D_MODEL), jnp.float32)
    meta_tokens = nrm(ks[1], (N_META, D_MODEL), jnp.float32)
    norm_w = 1.0 + 0.02 * nrm(ks[2], (DEPTH, D_MODEL), jnp.float32)
    w_in = nrm(ks[3], (DEPTH, D_MODEL, PROJ_WIDTH), jnp.float32) * D_MODEL ** -0.5
    conv_w = nrm(ks[4], (DEPTH, A_CONV, 2 * A_WK + A_WV), jnp.float32) * A_CONV ** -0.5
    a_log = jnp.log(jax.random.uniform(ks[5], (DEPTH, A_HEADS), jnp.float32, 1.0, 16.0))
    dt = jnp.exp(jax.random.uniform(ks[6], (DEPTH, A_HEADS), jnp.float32,
                                    math.log(1e-3), math.log(1e-1)))
    dt_bias = dt + jnp.log(-jnp.expm1(-dt))
    gnorm_a = 1.0 + 0.02 * nrm(ks[7], (DEPTH, A_DV), jnp.float32)
    gnorm_b = 1.0 + 0.02 * nrm(ks[8], (DEPTH, B_DV), jnp.float32)
    hgrn_lower_bounds = 0.1 * nrm(ks[9], (DEPTH, B_WK), jnp.float32)
    w_branch_a = nrm(ks[10], (DEPTH, A_WV, D_MODEL), jnp.float32) * A_WV ** -0.5
    w_branch_b = nrm(ks[11], (DEPTH, B_WV, D_MODEL), jnp.float32) * B_WV ** -0.5
    w_out = nrm(ks[12], (DEPTH, D_MODEL, D_MODEL), jnp.float32) * D_MODEL ** -0.5
    final_norm_w = 1.0 + 0.02 * nrm(ks[13], (D_MODEL,), jnp.float32)
    return {'x': x, 'meta_tokens': meta_tokens, 'norm_w': norm_w, 'w_in': w_in,
            'conv_w': conv_w, 'a_log': a_log, 'dt_bias': dt_bias, 'gnorm_a': gnorm_a,
            'gnorm_b': gnorm_b, 'hgrn_lower_bounds': hgrn_lower_bounds,
            'w_branch_a': w_branch_a, 'w_branch_b': w_branch_b, 'w_out': w_out,
            'final_norm_w': final_norm_w}


def reference(x, meta_tokens, norm_w, w_in, conv_w, a_log, dt_bias, gnorm_a, gnorm_b,
              hgrn_lower_bounds, w_branch_a, w_branch_b, w_out, final_norm_w):
    b = x.shape[0]
    meta = jnp.broadcast_to(meta_tokens.astype(x.dtype)[None], (b, N_META, D_MODEL))
    h = jnp.concatenate([meta, x], axis=1)
    lb_sm = jax.nn.softmax(hgrn_lower_bounds.astype(jnp.float32), axis=0)
    lb_all = jnp.cumsum(lb_sm, axis=0) - lb_sm[0]
    for l in range(DEPTH):
        h = _layer(h, norm_w[l], w_in[l], conv_w[l], a_log[l], dt_bias[l], gnorm_a[l],
                   gnorm_b[l], lb_all[l], w_branch_a[l], w_branch_b[l], w_out[l])
    return _rmsnorm(h, final_norm_w)[:, N_META:]
```

```python
import numpy as np
from contextlib import ExitStack
import concourse.bass as bass
import concourse.mybir as mybir
from concourse.bass_utils import run_bass_kernel_spmd

F32 = mybir.dt.float32
BF16 = mybir.dt.bfloat16
ALU = mybir.AluOpType
AF = mybir.ActivationFunctionType

EPOCH = 4000
NDMASEM = 16


class Buf:
    def __init__(self, t, name):
        self.t = t
        self.name = name

    def __getitem__(self, k):
        return self.t[k]


class _Op:
    __slots__ = ("eng", "fn", "deps", "signal", "dma", "epoch", "val", "slot", "idx", "rk", "wk")

    def __init__(self, eng, fn, dma):
        self.eng = eng
        self.fn = fn
        self.deps = []
        self.signal = False
        self.dma = dma
        self.epoch = 0
        self.val = 0
        self.slot = -1


class _Res:
    __slots__ = ("w", "r")

    def __init__(self):
        self.w = None
        self.r = []


class Prog:
    ENGS = ("pe", "act", "dve", "pool", "sp")

    def __init__(self, nc):
        self.nc = nc
        self.ops = []
        self.res = {}
        self.stack = ExitStack()
        self.nbank = 0
        self.banks = []

    def sb(self, name, shape, dt):
        t = self.stack.enter_context(self.nc.sbuf_tensor(name, list(shape), dt))
        return Buf(t, name)

    def ps(self, name, shape, dt):
        t = self.stack.enter_context(self.nc.psum_tensor(name, list(shape), dt))
        return Buf(t, name)

    def make_banks(self, n=8):
        self.banks = [self.ps("bank%d" % i, [128, 512], F32) for i in range(n)]

    def bank(self, pool=None):
        if pool is None:
            b = self.banks[self.nbank % len(self.banks)]
            self.nbank += 1
            return b
        cnt = self.__dict__.setdefault("_pc", {"a": 0, "b": 0})
        base = 0 if pool == "a" else 4
        b = self.banks[base + cnt[pool] % 4]
        cnt[pool] += 1
        return b

    def _keys(self, items):
        out = []
        for it in items:
            if isinstance(it, Buf):
                out.append((it.name, None))
            elif isinstance(it, tuple):
                b, k = it
                out.append((b.name if isinstance(b, Buf) else b, k))
            else:
                out.append((it, None))
        return out

    def _lookup(self, key):
        name, slot = key
        d = self.res.setdefault(name, {})
        if slot is None:
            return list(d.values()) if d else []
        out = []
        if None in d:
            out.append(d[None])
        if slot in d:
            out.append(d[slot])
        return out

    def _get(self, key):
        name, slot = key
        d = self.res.setdefault(name, {})
        if slot not in d:
            d[slot] = _Res()
        return d[slot]

    def op(self, eng, fn, reads=(), writes=(), dma=False):
        o = _Op(eng, fn, dma)
        deps = {}
        rk = self._keys(reads)
        wk = self._keys(writes)
        for key in rk:
            for r in self._lookup(key):
                if r.w is not None:
                    deps[id(r.w)] = r.w
                if key[0].startswith("bank"):
                    for x in r.r:
                        if x.eng != eng and id(x) not in deps:
                            deps[id(x)] = (x, "war")
        for key in wk:
            for r in self._lookup(key):
                if r.w is not None:
                    deps[id(r.w)] = r.w
                for x in r.r:
                    deps[id(x)] = (x, "war")
        for key in rk:
            rr = self._get(key)
            if not dma:
                rr.r = [x for x in rr.r if x.dma or x.eng != eng]
            rr.r.append(o)
        for key in wk:
            name, slot = key
            if slot is None:
                self.res[name] = {}
            r = self._get(key)
            r.w = o
            r.r = []
        for d in deps.values():
            war = False
            if isinstance(d, tuple):
                d, war = d[0], True
            if d is o:
                continue
            if d.eng == eng and not d.dma and not dma:
                if eng == "pe":
                    continue
                if war:
                    continue
            o.deps.append(d)
            d.signal = True
        if dma:
            o.signal = True
        self.ops.append(o)
        return o

    def pe(self, fn, r=(), w=()):
        return self.op("pe", fn, r, w)

    def act(self, fn, r=(), w=()):
        return self.op("act", fn, r, w)

    def dve(self, fn, r=(), w=()):
        return self.op("dve", fn, r, w)

    def pool(self, fn, r=(), w=()):
        return self.op("pool", fn, r, w)

    def dma(self, fn, r=(), w=(), q="sp"):
        return self.op(q, fn, r, w, dma=True)

    def emit(self):
        nc = self.nc
        ops = self.ops
        cnt = {e: 0 for e in self.ENGS}
        ndma = 0
        nsw = 0
        sw_ops = []
        slot_last = [None] * NDMASEM
        for o in ops:
            if o.dma and o.eng == "pool":
                o.slot = NDMASEM + nsw
                o.val = 16
                nsw += 1
                sw_ops.append(o)
            elif o.dma:
                o.slot = ndma % NDMASEM
                prev = slot_last[o.slot]
                o.val = (prev.val if prev is not None else 0) + 16
                if prev is not None:
                    o.deps.append(prev)
                slot_last[o.slot] = o
                ndma += 1
            elif o.signal:
                n = cnt[o.eng]
                o.epoch = n // EPOCH
                o.val = n % EPOCH + 1
                cnt[o.eng] = n + 1
        nep = {e: (cnt[e] + EPOCH - 1) // EPOCH + 1 for e in self.ENGS}
        print('signal counts', cnt, 'ndma', ndma, 'nops', len(ops))
        sems = {}
        for e in self.ENGS:
            sems[e] = [self.stack.enter_context(nc.semaphore("s_%s_%d" % (e, i))) for i in range(nep[e])]
        dsems = [self.stack.enter_context(nc.semaphore("s_dma_%d" % i)) for i in range(NDMASEM + nsw)]
        per = {e: [o for o in ops if o.eng == e] for e in self.ENGS}
        tails = [x for x in slot_last if x is not None] + sw_ops

        def run(eng_name, eng):
            waited = {}
            for o in per[eng_name]:
                for d in o.deps:
                    if d.dma:
                        k = ("dma", d.slot)
                        if waited.get(k, 0) < d.val:
                            eng.wait_ge(dsems[d.slot], d.val)
                            waited[k] = d.val
                    else:
                        k = d.eng
                        cur = waited.get(k, (-1, 0))
                        if cur < (d.epoch, d.val):
                            eng.wait_ge(sems[d.eng][d.epoch], d.val)
                            waited[k] = (d.epoch, d.val)
                ins = o.fn(eng)
                if o.dma:
                    ins.then_inc(dsems[o.slot], 16)
                elif o.signal:
                    ins.then_inc(sems[o.eng][o.epoch], 1)
            if eng_name == "sp":
                for d in tails:
                    k = ("dma", d.slot)
                    if waited.get(k, 0) < d.val:
                        eng.wait_ge(dsems[d.slot], d.val)
                        waited[k] = d.val

        with nc.Block() as block:
            @block.tensor
            def _(e):
                run("pe", e)

            @block.scalar
            def _(e):
                run("act", e)

            @block.vector
            def _(e):
                run("dve", e)

            @block.gpsimd
            def _(e):
                run("pool", e)

            @block.sync
            def _(e):
                run("sp", e)
        self.stack.close()


D = 1024
PW = 6152
NSEQ = 4
TPAD = 2176
EPS = 1e-6
QS = 128 ** -0.5
C_ID, C_U, C_U64, C_UR64, C_MNEG, C_STRICT, C_BLK, C_ONES = range(8)
SEG = dict(aq=0, ak=512, av=1024, ba=1536, az=1544, bq=2056, bf=2568, bi=3080, bg=3592, ga=4104, gb=5128)


def make_consts():
    c = np.zeros((128, 8, 128), np.float32)
    j = np.arange(128)[:, None]
    i = np.arange(128)[None, :]
    same = (j // 64) == (i // 64)
    c[:, C_ID] = (j == i)
    c[:, C_U] = (j <= i)
    c[:, C_U64] = (j <= i) & same
    c[:, C_UR64] = (j > i) & same
    c[:, C_MNEG] = np.where(j <= i, 0.0, -30000.0)
    c[:, C_STRICT] = (j < i)
    c[:, C_BLK] = (j <= i) & same
    c[:, C_ONES] = 1.0
    return c


class Ring:
    def __init__(self, P, nslots):
        self.P = P
        self.R = nslots
        self.slots = [P.sb("wr%d" % i, [128, 2048], BF16) for i in range(nslots)]
        self.sched = []
        self.issued = 0
        self.done_upto = -1
        self.doneset = set()
        self.cursor = 0

    def add(self, src, kc, ncols):
        self.sched.append((src, kc, ncols))

    def _pump(self):
        while self.issued < len(self.sched):
            m = self.issued
            if m - self.R > self.done_upto:
                break
            src, kc, ncols = self.sched[m]
            slot = self.slots[m % self.R]
            dst = slot[:, 0:kc * ncols].rearrange("p (k c) -> p k c", k=kc)
            self.P.dma(_dma(dst, src), w=[slot])
            self.issued += 1

    def get(self):
        n = self.cursor
        self.cursor += 1
        self._pump()
        assert n < self.issued, "ring schedule stall at chunk %d" % n
        src, kc, ncols = self.sched[n]
        slot = self.slots[n % self.R]
        return n, slot, slot[:, 0:kc * ncols].rearrange("p (k c) -> p k c", k=kc)

    def done(self, n):
        self.doneset.add(n)
        while (self.done_upto + 1) in self.doneset:
            self.done_upto += 1
        self._pump()


def _dma(dst, src):
    return lambda e: e.dma_start(out=dst, in_=src)


def build_program(nseq=NSEQ, layers=(0, 1), dbg=None, ngroups=5, stage=99):
    nc = bass.Bass("TRN2", target_bir_lowering=False)
    dt_in = lambda n, s: nc.dram_tensor(n, list(s), F32, kind="ExternalInput").ap()
    xin = dt_in("xin", [nseq, TPAD, D])
    cst_d = dt_in("cst", [128, 8, 128])
    normw_d = dt_in("normw", [128, 16])
    convw_d = dt_in("convw", [128, 96])
    gn_d = dt_in("gn", [128, 4])
    alog_d = dt_in("alog", [8])
    dtb_d = dt_in("dtb", [8])
    lbraw_d = dt_in("lbraw", [2, 512])
    fnw_d = dt_in("fnw", [D])
    w_in_d = dt_in("w_in", [2, D, PW])
    w_a_d = dt_in("w_a", [2, 512, D])
    w_b_d = dt_in("w_b", [2, 512, D])
    w_o_d = dt_in("w_o", [2, D, D])
    out_d = nc.dram_tensor("out", [nseq, 2048, D], F32, kind="ExternalOutput").ap()
    w_in_b = nc.dram_tensor("w_in_b", [2, D, PW], BF16, kind="Internal").ap()
    w_a_b = nc.dram_tensor("w_a_b", [2, 512, D], BF16, kind="Internal").ap()
    w_b_b = nc.dram_tensor("w_b_b", [2, 512, D], BF16, kind="Internal").ap()
    w_o_b = nc.dram_tensor("w_o_b", [2, D, D], BF16, kind="Internal").ap()

    st_Sa = nc.dram_tensor("st_Sa", [2, 128, 512], F32, kind="Internal").ap()
    st_Sb = nc.dram_tensor("st_Sb", [2, 128, 512], F32, kind="Internal").ap()
    st_Sab = nc.dram_tensor("st_Sab", [2, 128, 512], BF16, kind="Internal").ap()
    st_Sbb = nc.dram_tensor("st_Sbb", [2, 128, 512], BF16, kind="Internal").ap()
    st_hist = nc.dram_tensor("st_hist", [128, 72], BF16, kind="Internal").ap()
    P = Prog(nc)
    P.make_banks(8)
    sb = P.sb
    cst = sb("cst_s", [128, 8, 128], F32)
    identb = sb("identb", [128, 128], BF16)
    onesb = sb("onesb", [128, 128], BF16)
    normw = sb("normw_s", [128, 16], F32)
    convw = sb("convw_s", [128, 96], F32)
    gn = sb("gn_s", [128, 4], F32)
    nega = sb("nega", [128, 8], F32)
    dtb = sb("dtb_s", [128, 8], F32)
    lb1 = sb("lb1", [128, 512], F32)
    oml1 = sb("oml1", [128, 512], F32)
    fnw = sb("fnw_s", [128, D], F32)
    epst = sb("epst", [128, 1], F32)
    diagw = [sb("diagw%d" % l_, [128, 12, 4, 128], BF16) for l_ in range(2)]
    wba = sb("wba", [128, 8, 8], BF16)
    hT = sb("hT", [128, 8, 512], F32)
    sq = sb("sq", [128, 8, 512], BF16)
    xnT = sb("xnT", [128, 8, 512], BF16)
    rstd = sb("rstd", [128, 512], F32)
    rn = rstd
    pcv = [sb("pcv%d" % i, [128, 4, 515], BF16) for i in range(2)]
    hist = sb("hist", [128, 2, 12, 3], BF16)
    qkT = sb("qkT", [128, 8, 512], BF16)
    vT = sb("vT", [128, 4, 512], BF16)
    qbT = sb("qbT", [128, 4, 512], BF16)
    k_tm = sb("k_tm", [128, 4, 512], BF16)
    v_tm = sb("v_tm", [128, 4, 512], BF16)
    vb_tm = sb("vb_tm", [128, 4, 512], BF16)
    kk_tm = sb("kk_tm", [128, 4, 512], BF16)
    logf = sb("logf", [128, 4, 512], F32)
    oaT = sb("oaT", [128, 4, 512], BF16)
    obT = sb("obT", [128, 4, 512], BF16)
    beta = sb("beta", [128, 4, 4], F32)
    nbeta = sb("nbeta", [128, 4, 4], F32)
    gg = sb("gg", [128, 4, 4], F32)
    sm = [sb("sm%d" % i, [128, 4], F32) for i in range(6)]
    Fs = [sb("F%d" % i, [128, 512], F32) for i in range(9)]
    Bs = [sb("B%d" % i, [128, 512], BF16) for i in range(7)]
    Bs2 = [sb("B2_%d" % i, [128, 512], BF16) for i in range(5)]
    Sa = [sb("Sa%d" % l, [128, 4, 128], F32) for l in range(2)]
    Sab = [sb("Sab%d" % l, [128, 4, 128], BF16) for l in range(2)]
    Sb_ = [sb("Sb%d" % l, [128, 4, 128], F32) for l in range(2)]
    Sbb = [sb("Sbb%d" % l, [128, 4, 128], BF16) for l in range(2)]
    HF = [sb("HF%d" % i, [128, 512], F32) for i in range(4)]
    HB = [sb("HB%d" % i, [128, 512], BF16) for i in range(6)]
    ring = Ring(P, 6)

    def interleave(*gens):
        gens = list(gens)
        cnts = [0] * len(gens)
        idx = {id(g_): i for i, g_ in enumerate(gens)}
        wts = {id(g_): w_ for g_, w_ in zip(gens, (3, 2, 1, 1))}
        while gens:
            for g_ in list(gens):
                try:
                    for _ in range(wts[id(g_)]):
                        next(g_)
                        cnts[idx[id(g_)]] += 1
                except StopIteration:
                    gens.remove(g_)
        if _osk.environ.get("K_CNT"):
            print("yield counts", cnts)

    def mm(out, lhsT, rhs, st, sp, r, w):
        P.pe(lambda e: e.matmul(out, lhsT=lhsT, rhs=rhs, start=st, stop=sp), r, w)

    def tr(out, in_, ident, r, w):
        P.pe(lambda e: e.transpose(out, in_, ident), r, w)

    def tt(eng, out, in0, in1, op, r, w):
        P.op(eng, lambda e: e.tensor_tensor(out=out, in0=in0, in1=in1, op=op), r, w)

    def ts(eng, out, in0, s1, s2, op0, op1, r, w):
        if s2 is None:
            P.op(eng, lambda e: e.tensor_scalar(out=out, in0=in0, scalar1=s1, scalar2=None, op0=op0), r, w)
        else:
            P.op(eng, lambda e: e.tensor_scalar(out=out, in0=in0, scalar1=s1, scalar2=s2, op0=op0, op1=op1), r, w)

    def stt(eng, out, in0, scalar, in1, op0, op1, r, w):
        P.op(eng, lambda e: e.scalar_tensor_tensor(out=out, in0=in0, scalar=scalar, in1=in1, op0=op0, op1=op1), r, w)

    def cp(eng, out, in_, r, w):
        if eng == "act":
            P.act(lambda e: e.copy(out=out, in_=in_), r, w)
        else:
            P.op(eng, lambda e: e.tensor_copy(out=out, in_=in_), r, w)

    def af(out, in_, func, r, w, bias=None, scale=None):
        kw = {}
        if bias is not None:
            kw["bias"] = bias
        if scale is not None:
            kw["scale"] = scale
        P.act(lambda e: e.activation(out=out, in_=in_, func=func, **kw), r, w)

    def rsq(out, in_, scale, r, w):
        af(out, in_, AF.Ln, list(r) + [epst], w, bias=epst[:, 0:1], scale=scale)
        af(out, out, AF.Exp, w, w, scale=-0.5)

    def recip(out, in_, r, w):
        P.dve(lambda e: e.reciprocal(out=out, in_=in_), r, w)

    def v4(ap):
        return ap.rearrange("p (h i) -> p h i", h=4)

    def bc_mid(ap128):
        return ap128.unsqueeze(1).to_broadcast([128, 4, 128])

    def bc_last(ap4):
        return ap4.unsqueeze(2).to_broadcast([128, 4, 128])

    dumps = []
    import os as _osk
    _kstop = _osk.environ.get("K_STOP", "")

    class StopBuild(Exception):
        pass

    _kcnt = {}

    def ck(label):
        _kcnt[label] = _kcnt.get(label, 0) + 1
        lab, _, n = _kstop.partition(":")
        if lab == label and _kcnt[label] == int(n or 1):
            raise StopBuild()

    def dump(name, ap, shape, r):
        if dbg is None or name not in dbg:
            return
        d = nc.dram_tensor("dbg_" + name, list(shape), ap.dtype, kind="ExternalOutput").ap()
        P.dma(_dma(d, ap), r=r)
        dumps.append(name)

    P.dma(_dma(cst[:], cst_d), w=[cst])
    P.dma(_dma(normw[:], normw_d), w=[normw])
    P.dma(_dma(convw[:], convw_d), w=[convw])
    P.dma(_dma(gn[:], gn_d), w=[gn])
    P.dma(_dma(nega[:], alog_d.partition_broadcast(128)), w=[nega])
    P.dma(_dma(dtb[:], dtb_d.partition_broadcast(128)), w=[dtb])
    P.dma(_dma(Fs[0][:], lbraw_d[0].partition_broadcast(128)), w=[Fs[0]])
    P.dma(_dma(Fs[1][:], lbraw_d[1].partition_broadcast(128)), w=[Fs[1]])
    P.dma(_dma(fnw[:], fnw_d.partition_broadcast(128)), w=[fnw])
    P.dve(lambda e: e.memset(epst[:], EPS), w=[epst])
    cp("dve", identb[:], cst[:, C_ID, :], [cst], [identb])
    cp("dve", onesb[:], cst[:, C_ONES, :], [cst], [onesb])
    af(nega[:], nega[:], AF.Exp, [nega], [nega])
    ts("dve", nega[:], nega[:], -1.0, None, ALU.mult, None, [nega], [nega])
    tt("dve", lb1[:], Fs[1][:], Fs[0][:], ALU.subtract, [Fs[0], Fs[1]], [lb1])
    af(lb1[:], lb1[:], AF.Sigmoid, [lb1], [lb1])
    ts("dve", oml1[:], lb1[:], -1.0, 1.0, ALU.mult, ALU.add, [lb1], [oml1])
    for l in layers:
        for kc in range(8):
            P.dma(_dma(w_in_b[l, kc * 128:(kc + 1) * 128, :], w_in_d[l, kc * 128:(kc + 1) * 128, :]), w=[("w_in_b", l)], q="pool")
        for (wb_, wd_, nm) in ((w_o_b, w_o_d, "w_o_b"), (w_a_b, w_a_d, "w_a_b"), (w_b_b, w_b_d, "w_b_b")):
            P.dma(_dma(wb_[l], wd_[l]), w=[(nm, l)], q="pool")

    groups = ([(0, 1)] + [(128 + 512 * g, 4) for g in range(4)])[:ngroups]

    def w_in_cols(l, c0, n):
        return w_in_b[l].rearrange("(kc p) c -> p kc c", p=128)[:, :, c0:c0 + n]

    for s in range(nseq):
        for gi_, (t0, NT) in enumerate(groups):
            if gi_ == 0 and s > 0:
                continue
            for l in layers:
                for seg in ("aq", "ak", "av", "az", "bq", "bf", "bi", "bg"):
                    for half in range(2):
                        ring.add(w_in_cols(l, SEG[seg] + half * 256, 256), 8, 256)
                for qd in range(4):
                    ring.add(w_a_b[l].rearrange("(kc p) c -> p kc c", p=128)[:, :, qd * 256:(qd + 1) * 256], 4, 256)
                    ring.add(w_b_b[l].rearrange("(kc p) c -> p kc c", p=128)[:, :, qd * 256:(qd + 1) * 256], 4, 256)
                    ring.add(w_in_cols(l, SEG["ga"] + qd * 256, 256), 8, 256)
                    ring.add(w_in_cols(l, SEG["gb"] + qd * 256, 256), 8, 256)
                for qd in range(4):
                    ring.add(w_o_b[l].rearrange("(kc p) c -> p kc c", p=128)[:, :, qd * 256:(qd + 1) * 256], 8, 256)
    WDEP = [("w_in_b", l) for l in layers] + [("w_o_b", l) for l in layers] + [("w_a_b", l) for l in layers] + [("w_b_b", l) for l in layers]
    _orig_dma = P.dma

    def ring_dma(fn, r=(), w=(), q="sp"):
        return _orig_dma(fn, r=list(r) + WDEP, w=w, q=q)
    ring.P = type("X", (), {"dma": staticmethod(ring_dma)})()

    cur_layer_diag = [None]

    def build_diag(l):
        for c in range(12):
            for k in range(4):
                col = l * 48 + k * 12 + c
                ts("dve", diagw[l][:, c, k, :], cst[:, C_ID, :], convw[:, col:col + 1], None, ALU.mult, None, [cst, convw], [(diagw[l], c)])

    def rms_bcast(src_sq_of, nchunks, TG, scale):
        b = P.bank()
        for c in range(nchunks):
            mm(b[:, 0:TG], onesb[:], src_sq_of(c), c == 0, c == nchunks - 1, [sq, onesb], [b])
        return b

    for l_ in layers:
        build_diag(l_)

    def layer(l, TG, NT, first_group):
        c_start = ring.cursor

        def bail():
            while ring.cursor < c_start + 36:
                n_, _, _ = ring.get()
                ring.done(n_)
        if stage < 1:
            return bail()
        af(sq[:, :, 0:TG], hT[:, :, 0:TG], AF.Square, [hT], [sq])
        b = rms_bcast(lambda c: sq[:, c, 0:TG], 8, TG, 1.0 / D)
        rsq(rstd[:, 0:TG], b[:, 0:TG], 1.0 / D, [b], [rstd])
        for kc in range(8):
            stt("dve", xnT[:, kc, 0:TG], hT[:, kc, 0:TG], normw[:, l * 8 + kc:l * 8 + kc + 1], rstd[:, 0:TG],
                ALU.mult, ALU.mult, [hT, normw, rstd], [(xnT, kc)])
        if stage < 2:
            return bail()
        P.dma(_dma(wba[:], w_in_cols(l, SEG["ba"], 8)), r=WDEP, w=[wba])

        def proj_fm(evac):
            for half in range(2):
                n, slot, W = ring.get()
                for c2 in range(2):
                    bk = P.bank()
                    for kc in range(8):
                        mm(bk[:, 0:TG], W[:, kc, c2 * 128:(c2 + 1) * 128], xnT[:, kc, 0:TG], kc == 0, kc == 7, [slot, xnT], [bk])
                    evac(half * 2 + c2, bk)
                ring.done(n)

        def conv_all(segs):
            pend = [None]

            def make_conv(si, c, pb, dst_of):
                def f():
                    b2 = P.bank()
                    for k in range(4):
                        mm(b2[:, 0:TG], diagw[l][:, si * 4 + c, k, :], pb[:, c, k:k + TG], k == 0, k == 3, [(pb, c), diagw[l]], [b2])
                    af(dst_of(c), b2[:, 0:TG], AF.Silu, [b2], [dst_of.buf])
                return f
            for si, dst_of in segs:
                pb = pcv[si % 2]
                if first_group:
                    P.pool(lambda e, pb=pb: e.memset(pb[:, :, 0:3], 0.0), w=[pb])
                else:
                    cp("pool", pb[:, :, 0:3], hist[:, l, si * 4:(si + 1) * 4, :], [hist], [pb])
                for half in range(2):
                    n, slot, W = ring.get()
                    for c2 in range(2):
                        c = half * 2 + c2
                        bk = P.bank()
                        for kc in range(8):
                            mm(bk[:, 0:TG], W[:, kc, c2 * 128:(c2 + 1) * 128], xnT[:, kc, 0:TG], kc == 0, kc == 7, [slot, xnT], [bk])
                        cp("dve", pb[:, c, 3:3 + TG], bk[:, 0:TG], [bk], [(pb, c)])
                        if pend[0] is not None:
                            pend[0]()
                        pend[0] = make_conv(si, c, pb, dst_of)
                    ring.done(n)
                cp("pool", hist[:, l, si * 4:(si + 1) * 4, :], pb[:, :, TG:TG + 3], [pb], [hist])
            if pend[0] is not None:
                pend[0]()

        def dq(c): return qkT[:, c, 0:TG]
        dq.buf = qkT
        def dk(c): return qkT[:, 4 + c, 0:TG]
        dk.buf = qkT
        def dv(c): return vT[:, c, 0:TG]
        dv.buf = vT
        conv_all(((0, dq), (1, dk), (2, dv)))
        bba = P.bank()
        for j in range(NT):
            for kc in range(8):
                mm(bba[:, j * 8:(j + 1) * 8], xnT[:, kc, j * 128:(j + 1) * 128], wba[:, kc, :], kc == 0, kc == 7, [xnT, wba], [bba])
        bav = bba[:, 0:NT * 8].rearrange("p (j c) -> p j c", c=8)
        af(beta[:, 0:NT, :], bav[:, :, 0:4], AF.Sigmoid, [bba], [beta])
        ts("dve", nbeta[:, 0:NT, :], beta[:, 0:NT, :], -1.0, None, ALU.mult, None, [beta], [nbeta])
        tt("dve", gg[:, 0:NT, :], bav[:, :, 4:8], dtb[:, l * 4:(l + 1) * 4].unsqueeze(1).to_broadcast([128, NT, 4]), ALU.add, [bba, dtb], [gg])
        af(gg[:, 0:NT, :], gg[:, 0:NT, :], AF.Exp, [gg], [gg])
        af(gg[:, 0:NT, :], gg[:, 0:NT, :], AF.Ln, [gg], [gg], bias=1.0)
        tt("dve", gg[:, 0:NT, :], gg[:, 0:NT, :], nega[:, l * 4:(l + 1) * 4].unsqueeze(1).to_broadcast([128, NT, 4]), ALU.mult, [gg, nega], [gg])
        if TG == 512 and l == 0:
            dump("qkT", qkT[:, :, 0:TG], [128, 8, TG], [qkT])
            dump("xnT", xnT[:, :, 0:TG], [128, 8, TG], [xnT])
            dump("vT", vT[:, :, 0:TG], [128, 4, TG], [vT])
            dump("logf", logf[:], [128, 4, 512], [logf])
            dump("gg", gg[:], [128, 4, 4], [gg])
            dump("beta", beta[:], [128, 4, 4], [beta])
        if stage < 3:
            return bail()
        if stage < 4:
            return bail()
        cF = cst

        def gdn_gen():
            tt("dve", pcv[0][:, :, 0:TG], qkT[:, 0:4, 0:TG], qkT[:, 0:4, 0:TG], ALU.mult, [qkT], [pcv[0]])
            tt("dve", pcv[1][:, :, 0:TG], qkT[:, 4:8, 0:TG], qkT[:, 4:8, 0:TG], ALU.mult, [qkT], [pcv[1]])
            for c in range(8):
                bk = P.bank("a")
                mm(bk[:, 0:TG], onesb[:], pcv[c // 4][:, c % 4, 0:TG], True, True, [pcv[c // 4], onesb], [bk])
                yield
                rsq(rn[:, 0:TG], bk[:, 0:TG], 1.0, [bk], [rn])
                yield
                stt("dve", qkT[:, c, 0:TG], qkT[:, c, 0:TG], QS if c < 4 else 1.0, rn[:, 0:TG], ALU.mult, ALU.mult, [qkT, rn], [qkT])
                yield
            for j in range(NT):
                for (src, c0, dst) in ((qkT, 4, k_tm), (vT, 0, v_tm)):
                    bk = P.bank("a")
                    bv = bk[:].bitcast(BF16)
                    for h in range(4):
                        tr(bv[:, h * 128:(h + 1) * 128], src[:, c0 + h, j * 128:(j + 1) * 128], identb[:], [src, identb], [bk])
                    yield
                    cp("dve", dst[:, j, :], bv[:, 0:512], [bk], [(dst, j)])
                    yield
            Bset = [Bs[0:5], Bs2[0:5]]

            def genX(j):
                tj = slice(j * 128, (j + 1) * 128)
                yield
                g_j = gg[:, j, :]
                yield
                F0, F1, F2, F3, F4, F5, F6, F7, F8 = Fs[0:9]
                yield
                B0, B1, B2, B3, B4 = Bset[j % 2]
                yield
                Gs, tmp4, kde, eG = sm[0:4]
                egl = sm[4 + j % 2]
                yield
                tt("dve", v4(F0[:]), bc_mid(cF[:, C_U, :]), bc_last(g_j), ALU.mult, [cF, gg], [F0])
                yield
                bG = P.bank("a")
                yield
                mm(bG[:, :], cF[:, C_ONES, :], F0[:], True, True, [cF, F0], [bG])
                yield
                bGs = P.bank("a")
                yield
                mm(bGs[:, 0:4], cF[:, C_U, :], g_j, True, True, [cF, gg], [bGs])
                yield
                cp("act", Gs[:], bGs[:, 0:4], [bGs], [Gs])
                yield
                glast = v4(bG[:, :])[:, :, 127]
                yield
                tt("dve", tmp4[:], glast, Gs[:], ALU.subtract, [bG, Gs], [tmp4])
                yield
                af(kde[:], tmp4[:], AF.Exp, [tmp4], [kde])
                yield
                af(eG[:], Gs[:], AF.Exp, [Gs], [eG])
                yield
                af(egl[:], glast, AF.Exp, [bG], [egl])
                yield
                ck("g_a")
                yield
                tt("dve", v4(F1[:]), v4(bG[:, :]), bc_last(Gs[:]), ALU.subtract, [bG, Gs], [F1])
                yield
                ck("b1")
                yield
                tt("dve", v4(F1[:]), v4(F1[:]), bc_mid(cF[:, C_MNEG, :]), ALU.add, [F1, cF], [F1])
                yield
                ck("b2")
                yield
                af(F1[:], F1[:], AF.Exp, [F1], [F1])
                yield
                ck("b3")
                yield
                af(F2[:], bG[:, :], AF.Exp, [bG], [F2])
                yield
                ck("b4")
                yield
                for h in range(4):
                    tt("dve", B0[:, h * 128:(h + 1) * 128], qkT[:, h, tj], F2[:, h * 128:(h + 1) * 128], ALU.mult, [qkT, F2], [B0])
                yield
                ck("g_b")
                yield
                bKK = P.bank("a")
                yield
                bQK = P.bank("a")
                yield
                for h in range(4):
                    hs = slice(h * 128, (h + 1) * 128)
                    mm(bKK[:, hs], qkT[:, 4 + h, tj], qkT[:, 4 + h, tj], True, True, [qkT], [bKK])
                yield
                for h in range(4):
                    hs = slice(h * 128, (h + 1) * 128)
                    mm(bQK[:, hs], qkT[:, 4 + h, tj], qkT[:, h, tj], True, True, [qkT], [bQK])
                yield
                tt("dve", B1[:], bQK[:, :], F1[:], ALU.mult, [bQK, F1], [B1])
                yield
                tt("dve", v4(F4[:]), bc_mid(cF[:, C_STRICT, :]), bc_last(nbeta[:, j, :]), ALU.mult, [cF, nbeta], [F4])
                yield
                tt("dve", F3[:], bKK[:, :], F1[:], ALU.mult, [bKK, F1], [F3])
                yield
                tt("dve", F3[:], F3[:], F4[:], ALU.mult, [F3, F4], [F3])
                yield
                ck("g_c")
                yield
                bT = P.bank("a")
                yield
                for h in range(4):
                    hs = slice(h * 128, (h + 1) * 128)
                    tr(bT[:, hs], F3[:, hs], cF[:, C_ID, :], [F3, cF], [bT])
                yield
                cp("act", F5[:], bT[:, :], [bT], [F5])
                yield
                tt("dve", v4(F6[:]), v4(F3[:]), bc_mid(cF[:, C_ID, :]), ALU.add, [F3, cF], [F6])
                yield
                ck("g_d")
                yield
                A, AT, A2, AT2 = F3, F5, F7, F8
                yield
                for it in range(6):
                    b2 = P.bank("a")
                    b1 = P.bank("a")
                    for h in range(4):
                        hs = slice(h * 128, (h + 1) * 128)
                        mm(b2[:, hs], A[:, hs], AT[:, hs], True, True, [A, AT], [b2])
                    cp("dve", AT2[:], b2[:, :], [b2], [AT2])
                    yield
                    if it < 5:
                        for h in range(4):
                            hs = slice(h * 128, (h + 1) * 128)
                            mm(b1[:, hs], AT[:, hs], A[:, hs], True, True, [A, AT], [b1])
                        cp("act", A2[:], b1[:, :], [b1], [A2])
                    yield
                    b3 = P.bank("a")
                    for h in range(4):
                        hs = slice(h * 128, (h + 1) * 128)
                        mm(b3[:, hs], AT2[:, hs], F6[:, hs], True, True, [AT2, F6], [b3])
                    yield
                    tt("dve", F6[:], F6[:], b3[:, :], ALU.add, [F6, b3], [F6])
                    yield
                    A, AT, A2, AT2 = A2, AT2, A, AT
                yield
                cp("act", B2[:], F6[:], [F6], [B2])
                yield
                ck("g_e")
                yield
                tt("dve", v4(B3[:]), v4(k_tm[:, j, :]), bc_last(eG[:]), ALU.mult, [(k_tm, j), eG], [B3])
                yield
                tt("dve", v4(B4[:]), v4(k_tm[:, j, :]), bc_last(kde[:]), ALU.mult, [(k_tm, j), kde], [B4])
                yield

            def genY(j):
                tj = slice(j * 128, (j + 1) * 128)
                B0, B1, B2, B3, B4 = Bset[j % 2]
                B5, B6 = Bs[5], Bs[6]
                egl = sm[4 + j % 2]
                bW = P.bank("a")
                yield
                for h in range(4):
                    hs = slice(h * 128, (h + 1) * 128)
                    mm(bW[:, hs], B3[:, hs], B2[:, hs], True, True, [B3, B2], [bW])
                yield
                ts("dve", B5[:], bW[:, :], -1.0, None, ALU.mult, None, [bW], [B5])
                yield
                bV = P.bank("a")
                yield
                for h in range(4):
                    hs = slice(h * 128, (h + 1) * 128)
                    mm(bV[:, hs], B2[:, hs], v_tm[:, j, hs], True, False, [B2, (v_tm, j)], [bV])
                    mm(bV[:, hs], B5[:, hs], Sab[l][:, h, :], False, True, [B5, Sab[l]], [bV])
                yield
                tt("dve", v4(B6[:]), v4(bV[:, :]), bc_last(beta[:, j, :]), ALU.mult, [bV, beta], [B6])
                yield
                ck("g_f")
                yield
                bO = P.bank("a")
                yield
                for h in range(4):
                    hs = slice(h * 128, (h + 1) * 128)
                    mm(bO[:, hs], Sab[l][:, h, :], B0[:, hs], True, False, [Sab[l], B0], [bO])
                    mm(bO[:, hs], B6[:, hs], B1[:, hs], False, True, [B6, B1], [bO])
                yield
                cp("act", oaT[:, :, tj], v4(bO[:, :]), [bO], [(oaT, j)])
                yield
                ck("g_g")
                yield
                bS = P.bank("a")
                yield
                for h in range(4):
                    hs = slice(h * 128, (h + 1) * 128)
                    mm(bS[:, hs], B4[:, hs], B6[:, hs], True, True, [B4, B6], [bS])
                yield
                tt("dve", Sa[l][:], Sa[l][:], bc_last(egl[:]), ALU.mult, [Sa[l], egl], [Sa[l]])
                yield
                tt("dve", Sa[l][:], Sa[l][:], v4(bS[:, :]), ALU.add, [Sa[l], bS], [Sa[l]])
                yield
                cp("act", Sab[l][:], Sa[l][:], [Sa[l]], [Sab[l]])
                yield
                ck("g_h")

            def merge2(ga, gb):
                gens_ = [ga, gb]
                while gens_:
                    for g_ in list(gens_):
                        try:
                            next(g_)
                            yield
                        except StopIteration:
                            gens_.remove(g_)
            yield from genX(0)
            for j in range(NT):
                if j + 1 < NT:
                    yield from merge2(genY(j), genX(j + 1))
                else:
                    yield from genY(j)

            tt("dve", pcv[0][:, :, 0:TG], oaT[:, :, 0:TG], oaT[:, :, 0:TG], ALU.mult, [oaT], [pcv[0]])
            yield
            for h in range(4):
                bk = P.bank("a")
                mm(bk[:, 0:TG], onesb[:], pcv[0][:, h, 0:TG], True, True, [pcv[0], onesb], [bk])
                yield
                rsq(Fs[0][:, 0:TG], bk[:, 0:TG], 1.0 / 128, [bk], [Fs[0]])
                yield
                stt("dve", vT[:, h, 0:TG], oaT[:, h, 0:TG], gn[:, l * 2:l * 2 + 1], Fs[0][:, 0:TG], ALU.mult, ALU.mult, [oaT, gn, Fs[0]], [vT])
                yield
                tt("dve", vT[:, h, 0:TG], vT[:, h, 0:TG], sq[:, h, 0:TG], ALU.mult, [vT, (sq, h)], [vT])
                yield

        def hg_gen():
            for zoff_ in (0,):
                for half in range(2):
                    n, slot, W = ring.get()
                    for c2 in range(2):
                        bk = P.bank("b")
                        for kc in range(8):
                            mm(bk[:, 0:TG], W[:, kc, c2 * 128:(c2 + 1) * 128], xnT[:, kc, 0:TG], kc == 0, kc == 7, [slot, xnT], [bk])
                            if kc % 2 == 1:
                                yield
                        af(sq[:, zoff_ + half * 2 + c2, 0:TG], bk[:, 0:TG], AF.Silu, [bk], [(sq, zoff_ + half * 2 + c2)])
                    ring.done(n)

            for half in range(2):
                n, slot, W = ring.get()
                for c2 in range(2):
                    bk = P.bank("b")
                    for kc in range(8):
                        mm(bk[:, 0:TG], W[:, kc, c2 * 128:(c2 + 1) * 128], xnT[:, kc, 0:TG], kc == 0, kc == 7, [slot, xnT], [bk])
                        if kc % 2 == 1:
                            yield
                    af(qbT[:, half * 2 + c2, 0:TG], bk[:, 0:TG], AF.Silu, [bk], [qbT])
                ring.done(n)
            for which in ("bf", "bi"):
                bks = [P.bank("b") for _ in range(NT)]
                for half in range(2):
                    n, slot, W = ring.get()
                    for j in range(NT):
                        for kc in range(8):
                            mm(bks[j][:, half * 256:(half + 1) * 256], xnT[:, kc, j * 128:(j + 1) * 128], W[:, kc, :], kc == 0, kc == 7, [slot, xnT], [bks[j]])
                            if kc % 2 == 1:
                                yield
                    ring.done(n)
                for j in range(NT):
                    yield
                    if which == "bf":
                        F = HF[j % 2]
                        af(F[:], bks[j][:], AF.Sigmoid, [bks[j]], [F])
                        if l > 0:
                            tt("dve", F[:], F[:], oml1[:], ALU.mult, [F, oml1], [F])
                            tt("dve", F[:], F[:], lb1[:], ALU.add, [F, lb1], [F])
                        af(logf[:, j, :], F[:], AF.Ln, [F], [(logf, j)])
                        ts("dve", kk_tm[:, j, :], F[:], -1.0, 1.0, ALU.mult, ALU.add, [F], [(kk_tm, j)])
                    else:
                        cp("act", vb_tm[:, j, :], bks[j][:], [bks[j]], [(vb_tm, j)])
            for j in range(NT):
                tj = slice(j * 128, (j + 1) * 128)
                F0, F1, F2, F3 = HF[0:4]
                yield
                B0, B1, B2, B3, B4, B5 = HB[0:6]
                yield
                bBc = P.bank("b")
                yield
                bBr = P.bank("b")
                yield
                bBT = P.bank("b")
                yield
                mm(bBc[:, :], cF[:, C_U64, :], logf[:, j, :], True, True, [cF, (logf, j)], [bBc])
                yield
                mm(bBr[:, :], cF[:, C_UR64, :], logf[:, j, :], True, True, [cF, (logf, j)], [bBr])
                yield
                for h in range(4):
                    hs = slice(h * 128, (h + 1) * 128)
                    mm(bBT[:, hs], logf[:, j, hs], cF[:, C_U64, :], True, True, [cF, (logf, j)], [bBT])
                yield
                af(F0[:], bBc[:, :], AF.Exp, [bBc], [F0], scale=-1.0)
                yield
                af(F1[:], bBr[:, :], AF.Exp, [bBr], [F1])
                yield
                af(F2[:], bBT[:, :], AF.Exp, [bBT], [F2])
                yield
                tt("dve", B0[:], kk_tm[:, j, :], F0[:], ALU.mult, [(kk_tm, j), F0], [B0])
                yield
                tt("dve", B1[:], kk_tm[:, j, :], F1[:], ALU.mult, [(kk_tm, j), F1], [B1])
                yield
                stt("dve", v4(B2[:]), qbT[:, :, tj], QS, v4(F2[:]), ALU.mult, ALU.mult, [qbT, F2], [B2])
                yield
                bKT = P.bank("b")
                yield
                bKTv = bKT[:].bitcast(BF16)
                yield
                for h in range(4):
                    hs = slice(h * 128, (h + 1) * 128)
                    tr(bKTv[:, hs], B0[:, hs], identb[:], [B0, identb], [bKT])
                yield
                cp("dve", B3[:], bKTv[:, 0:512], [bKT], [B3])
                yield
                bSc = P.bank("b")
                yield
                for h in range(4):
                    hs = slice(h * 128, (h + 1) * 128)
                    mm(bSc[:, hs], B3[:, hs], B2[:, hs], True, True, [B3, B2], [bSc])
                yield
                tt("dve", v4(B4[:]), v4(bSc[:, :]), bc_mid(cF[:, C_BLK, :]), ALU.mult, [bSc, cF], [B4])
                yield
                bS1 = P.bank("b")
                yield
                for h in range(4):
                    hs = slice(h * 128, (h + 1) * 128)
                    mm(bS1[:, hs], B1[0:64, hs], vb_tm[0:64, j, hs], True, True, [B1, (vb_tm, j)], [bS1])
                yield
                e63 = v4(F2[:])[:, :, 63]
                yield
                e127 = v4(F2[:])[:, :, 127]
                yield
                tt("dve", v4(F3[:]), Sb_[l][:], bc_last(e63), ALU.mult, [Sb_[l], F2], [F3])
                yield
                tt("dve", F3[:], F3[:], bS1[:, :], ALU.add, [F3, bS1], [F3])
                yield
                cp("act", B5[:], F3[:], [F3], [B5])
                yield
                bO = P.bank("b")
                yield
                for h in range(4):
                    hs = slice(h * 128, (h + 1) * 128)
                    mm(bO[:, hs], vb_tm[:, j, hs], B4[:, hs], True, False, [(vb_tm, j), B4], [bO])
                    mm(bO[:, h * 128:h * 128 + 64], Sbb[l][:, h, :], B2[:, h * 128:h * 128 + 64], False, False, [Sbb[l], B2], [bO])
                    mm(bO[:, h * 128 + 64:h * 128 + 128], B5[:, hs], B2[:, h * 128 + 64:h * 128 + 128], False, True, [B5, B2], [bO])
                yield
                cp("act", obT[:, :, tj], v4(bO[:, :]), [bO], [(obT, j)])
                yield
                bS2 = P.bank("b")
                yield
                for h in range(4):
                    hs = slice(h * 128, (h + 1) * 128)
                    mm(bS2[:, hs], B1[64:128, hs], vb_tm[64:128, j, hs], True, True, [B1, (vb_tm, j)], [bS2])
                yield
                tt("dve", Sb_[l][:], v4(F3[:]), bc_last(e127), ALU.mult, [F3, F2], [Sb_[l]])
                yield
                tt("dve", Sb_[l][:], Sb_[l][:], v4(bS2[:, :]), ALU.add, [Sb_[l], bS2], [Sb_[l]])
                yield
                cp("act", Sbb[l][:], Sb_[l][:], [Sb_[l]], [Sbb[l]])

            for zoff_ in (4,):
                for half in range(2):
                    n, slot, W = ring.get()
                    for c2 in range(2):
                        bk = P.bank("b")
                        for kc in range(8):
                            mm(bk[:, 0:TG], W[:, kc, c2 * 128:(c2 + 1) * 128], xnT[:, kc, 0:TG], kc == 0, kc == 7, [slot, xnT], [bk])
                            if kc % 2 == 1:
                                yield
                        af(sq[:, zoff_ + half * 2 + c2, 0:TG], bk[:, 0:TG], AF.Silu, [bk], [(sq, zoff_ + half * 2 + c2)])
                    ring.done(n)

            tt("dve", pcv[1][:, :, 0:TG], obT[:, :, 0:TG], obT[:, :, 0:TG], ALU.mult, [obT], [pcv[1]])
            yield
            for h in range(4):
                bk = P.bank("b")
                mm(bk[:, 0:TG], onesb[:], pcv[1][:, h, 0:TG], True, True, [pcv[1], onesb], [bk])
                yield
                rsq(rstd[:, 0:TG], bk[:, 0:TG], 1.0 / 128, [bk], [rstd])
                yield
                stt("dve", qbT[:, h, 0:TG], obT[:, h, 0:TG], gn[:, l * 2 + 1:l * 2 + 2], rstd[:, 0:TG], ALU.mult, ALU.mult, [obT, gn, rstd], [qbT])
                yield
                tt("dve", qbT[:, h, 0:TG], qbT[:, h, 0:TG], sq[:, 4 + h, 0:TG], ALU.mult, [qbT, (sq, 4 + h)], [qbT])
                yield

        interleave(gdn_gen(), hg_gen())
        if stage < 6:
            return bail()
        if stage < 7:
            return bail()
        yaT, ybT = vT, qbT
        if stage < 8:
            return bail()
        for qd in range(4):
            na, sa_, Wa = ring.get()
            nb, sb__, Wb = ring.get()
            nga, sga, Wga = ring.get()
            ngb, sgb, Wgb = ring.get()
            for c2 in range(2):
                e_ = qd * 2 + c2
                cs = slice(c2 * 128, (c2 + 1) * 128)
                bA = P.bank()
                bB = P.bank()
                bGa = P.bank()
                bGb = P.bank()
                for kc in range(4):
                    mm(bA[:, 0:TG], Wa[:, kc, cs], yaT[:, kc, 0:TG], kc == 0, kc == 3, [sa_, yaT], [bA])
                for kc in range(4):
                    mm(bB[:, 0:TG], Wb[:, kc, cs], ybT[:, kc, 0:TG], kc == 0, kc == 3, [sb__, ybT], [bB])
                for kc in range(8):
                    mm(bGa[:, 0:TG], Wga[:, kc, cs], xnT[:, kc, 0:TG], kc == 0, kc == 7, [sga, xnT], [bGa])
                for kc in range(8):
                    mm(bGb[:, 0:TG], Wgb[:, kc, cs], xnT[:, kc, 0:TG], kc == 0, kc == 7, [sgb, xnT], [bGb])
                Fa, Fb = Fs[(e_ % 2) * 2], Fs[(e_ % 2) * 2 + 1]
                af(Fa[:, 0:TG], bGa[:, 0:TG], AF.Sigmoid, [bGa], [Fa])
                af(Fb[:, 0:TG], bGb[:, 0:TG], AF.Sigmoid, [bGb], [Fb])
                tt("dve", Fa[:, 0:TG], Fa[:, 0:TG], bA[:, 0:TG], ALU.mult, [Fa, bA], [Fa])
                tt("dve", Fb[:, 0:TG], Fb[:, 0:TG], bB[:, 0:TG], ALU.mult, [Fb, bB], [Fb])
                tt("dve", sq[:, e_, 0:TG], Fa[:, 0:TG], Fb[:, 0:TG], ALU.add, [Fa, Fb], [(sq, e_)])
            ring.done(na); ring.done(nb); ring.done(nga); ring.done(ngb)
        if stage < 9:
            return bail()
        for qd in range(4):
            n, slot, W = ring.get()
            for c2 in range(2):
                e_ = qd * 2 + c2
                bk = P.bank()
                for kc in range(8):
                    mm(bk[:, 0:TG], W[:, kc, c2 * 128:(c2 + 1) * 128], sq[:, kc, 0:TG], kc == 0, kc == 7, [slot, sq], [bk])
                tt("dve", hT[:, e_, 0:TG], hT[:, e_, 0:TG], bk[:, 0:TG], ALU.add, [(hT, e_), bk], [(hT, e_)])
            ring.done(n)

    def main_loop():
        def flat(b_):
            return b_[:].rearrange("p h i -> p (h i)")
        for s in range(nseq):
            if s == 0:
                for l in layers:
                    P.dve(lambda e, l=l: e.memset(Sa[l][:], 0.0), w=[Sa[l]])
                    P.dve(lambda e, l=l: e.memset(Sab[l][:], 0.0), w=[Sab[l]])
                    P.pool(lambda e, l=l: e.memset(Sb_[l][:], 0.0), w=[Sb_[l]])
                    P.pool(lambda e, l=l: e.memset(Sbb[l][:], 0.0), w=[Sbb[l]])
            else:
                for l in layers:
                    P.dma(_dma(flat(Sa[l]), st_Sa[l]), r=["st"], w=[Sa[l]])
                    P.dma(_dma(flat(Sab[l]), st_Sab[l]), r=["st"], w=[Sab[l]])
                    P.dma(_dma(flat(Sb_[l]), st_Sb[l]), r=["st"], w=[Sb_[l]])
                    P.dma(_dma(flat(Sbb[l]), st_Sbb[l]), r=["st"], w=[Sbb[l]])
                P.dma(_dma(hist[:].rearrange("p a b c -> p (a b c)"), st_hist), r=["st"], w=[hist])
            for gi, (t0, NT) in enumerate(groups):
                if gi == 0 and s > 0:
                    continue
                TG = NT * 128
                def xstage(j_):
                    return (Fs[5], Fs[6]) if j_ % 2 == 0 else (Fs[7], Fs[8])

                def xload(j_):
                    xh_ = xstage(j_)
                    for hx in range(2):
                        P.dma(_dma(xh_[hx][:], xin[s, t0 + j_ * 128:t0 + (j_ + 1) * 128, hx * 512:(hx + 1) * 512]), w=[xh_[hx]])
                xload(0)
                for j in range(NT):
                    if j + 1 < NT:
                        xload(j + 1)
                    xh = xstage(j)
                    for half in range(2):
                        bk = P.bank()
                        for k4 in range(4):
                            kc = half * 4 + k4
                            tr(bk[:, k4 * 128:(k4 + 1) * 128], xh[half][:, k4 * 128:(k4 + 1) * 128], cst[:, C_ID, :], [xh[half], cst], [bk])
                        cp("act" if half == 0 else "dve", hT[:, half * 4:(half + 1) * 4, j * 128:(j + 1) * 128], v4(bk[:, :]), [bk], [hT])
                for l in layers:
                    layer(l, TG, NT, gi == 0)
                    if dbg is not None and s == 0:
                        dump("h_l%d_g%d" % (l, gi), hT[:, :, 0:TG], [128, 8, TG], [hT])
                if gi == 0:
                    if s == 0 and nseq > 1:
                        for l in layers:
                            P.dma(_dma(st_Sa[l], flat(Sa[l])), r=[Sa[l]], w=["st"])
                            P.dma(_dma(st_Sab[l], flat(Sab[l])), r=[Sab[l]], w=["st"])
                            P.dma(_dma(st_Sb[l], flat(Sb_[l])), r=[Sb_[l]], w=["st"])
                            P.dma(_dma(st_Sbb[l], flat(Sbb[l])), r=[Sbb[l]], w=["st"])
                        P.dma(_dma(st_hist, hist[:].rearrange("p a b c -> p (a b c)")), r=[hist], w=["st"])
                    continue
                af(sq[:, :, 0:TG], hT[:, :, 0:TG], AF.Square, [hT], [sq])
                b = rms_bcast(lambda c: sq[:, c, 0:TG], 8, TG, 1.0 / D)
                rsq(rstd[:, 0:TG], b[:, 0:TG], 1.0 / D, [b], [rstd])
                tt("dve", hT[:, :, 0:TG], hT[:, :, 0:TG], rstd[:, 0:TG].unsqueeze(1).to_broadcast([128, 8, TG]), ALU.mult, [hT, rstd], [hT])
                for j in range(NT):
                    for half in range(2):
                        bk = P.bank()
                        for k4 in range(4):
                            kc = half * 4 + k4
                            tr(bk[:, k4 * 128:(k4 + 1) * 128], hT[:, kc, j * 128:(j + 1) * 128], cst[:, C_ID, :], [hT, cst], [bk])
                        yh = ((Fs[7], Fs[8]) if j % 2 == 0 else (Fs[5], Fs[6]))[half]
                        tt("dve", yh[:], bk[:, :], fnw[:, half * 512:(half + 1) * 512], ALU.mult, [bk, fnw], [yh])
                        r0 = t0 - 128 + j * 128
                        P.dma(_dma(out_d[s, r0:r0 + 128, half * 512:(half + 1) * 512], yh[:]), r=[yh])
    try:
        main_loop()
    except StopBuild:
        pass
    assert ring.cursor == len(ring.sched) or stage < 99 or _kstop, (ring.cursor, len(ring.sched))
    P.emit()
    return nc, dumps


def host_inputs(x, meta_tokens, norm_w, w_in, conv_w, a_log, dt_bias, gnorm_a, gnorm_b,
                hgrn_lower_bounds, w_branch_a, w_branch_b, w_out, final_norm_w, ncores=8, nseq=NSEQ):
    f = lambda a: np.ascontiguousarray(np.asarray(a, dtype=np.float32))
    x = f(x)
    shared = dict(
        cst=make_consts(),
        normw=f(f(norm_w).reshape(2, 8, 128).transpose(2, 0, 1).reshape(128, 16)),
        convw=f(f(conv_w).reshape(2, 4, 12, 128).transpose(3, 0, 1, 2).reshape(128, 96)),
        gn=f(np.stack([f(gnorm_a), f(gnorm_b)], axis=1).transpose(2, 0, 1).reshape(128, 4)),
        alog=f(a_log).reshape(8), dtb=f(dt_bias).reshape(8),
        lbraw=f(hgrn_lower_bounds), fnw=f(final_norm_w),
        w_in=f(w_in), w_a=f(w_branch_a), w_b=f(w_branch_b), w_o=f(w_out),
    )
    maps = []
    for c in range(ncores):
        xp = np.zeros((nseq, TPAD, D), np.float32)
        xp[:, 112:128] = f(meta_tokens)[None]
        xp[:, 128:] = x[c * nseq:(c + 1) * nseq]
        m = dict(shared)
        m["xin"] = xp
        maps.append(m)
    return maps


def kernel(**inputs):
    nc, _ = build_program()
    maps = host_inputs(**inputs)
    res = run_bass_kernel_spmd(nc, maps, core_ids=list(range(8)))
    return np.concatenate([np.asarray(r["out"]) for r in res.results], axis=0).astype(np.float32)
```
